# Optimizing a Trainium2 kernel written in Bass

```python
import math
import jax, jax.numpy as jnp
from jax import lax
import numpy as np

D_MODEL = 2048
BATCH = 1
SEQ = 8192
DEPTH = 4

CHUNK = 64
D_MIX = 2 * D_MODEL
SSD_WIDTH = D_MIX // 2
SBA_WIDTH = D_MIX - SSD_WIDTH
SSD_HEAD_DIM = 64
SSD_HEADS = SSD_WIDTH // SSD_HEAD_DIM
SSD_GROUPS = 4
SSD_STATE = 128
SSD_CONV = 4
SBA_HEAD_DIM = 128
SBA_HEADS = SBA_WIDTH // SBA_HEAD_DIM
Q_BLOCK = 128
EPS = 1e-6
CONV_DIM = SSD_WIDTH + 2 * SSD_GROUPS * SSD_STATE
IN_SPLITS = (
    SSD_WIDTH,
    SSD_WIDTH + CONV_DIM,
    SSD_WIDTH + CONV_DIM + SSD_HEADS,
    SSD_WIDTH + CONV_DIM + SSD_HEADS + SBA_WIDTH,
    SSD_WIDTH + CONV_DIM + SSD_HEADS + 2 * SBA_WIDTH,
    SSD_WIDTH + CONV_DIM + SSD_HEADS + 3 * SBA_WIDTH,
)
IN_COLS = SSD_WIDTH + CONV_DIM + SSD_HEADS + 4 * SBA_WIDTH

kernel_name = "hybrid_ssd_stickbreaking_parallel_heads"


def rmsnorm(x, w):
    xf = x.astype(jnp.float32)
    y = xf * lax.rsqrt(jnp.mean(xf * xf, axis=-1, keepdims=True) + EPS)
    return (y * w.astype(jnp.float32)).astype(x.dtype)


def causal_depthwise_conv(u, w, b):
    k_taps = w.shape[0]
    seq = u.shape[1]
    up = jnp.pad(u, ((0, 0), (k_taps - 1, 0), (0, 0)))
    out = b
    for j in range(k_taps):
        out = out + w[j] * up[:, j:j + seq]
    return out


def ssd_chunked_scan(x, dt, a_head, bm, cm):
    bsz, seq, n_heads, hd = x.shape
    g, n = bm.shape[2], bm.shape[3]
    r = n_heads // g
    nc = seq // CHUNK
    xs = (x.astype(jnp.float32) * dt[..., None]).reshape(bsz, nc, CHUNK, g, r, hd)
    da = (dt * a_head).reshape(bsz, nc, CHUNK, g, r).transpose(0, 3, 4, 1, 2)
    bc = bm.astype(jnp.float32).reshape(bsz, nc, CHUNK, g, n)
    cc = cm.astype(jnp.float32).reshape(bsz, nc, CHUNK, g, n)
    a_cum = jnp.cumsum(da, axis=-1)
    tri = jnp.tril(jnp.ones((CHUNK, CHUNK), dtype=bool))
    seg = a_cum[..., :, None] - a_cum[..., None, :]
    decay_in = jnp.exp(jnp.where(tri, seg, -jnp.inf))
    cb = jnp.einsum('bclgn,bcsgn->bgcls', cc, bc)
    y_diag = jnp.einsum('bgcls,bgrcls,bcsgrp->bclgrp', cb, decay_in, xs)
    decay_states = jnp.exp(a_cum[..., -1:] - a_cum)
    states = jnp.einsum('bclgn,bgrcl,bclgrp->cbgrpn', bc, decay_states, xs)
    chunk_decay = jnp.exp(a_cum[..., -1]).transpose(3, 0, 1, 2)

    def step(h, inp):
        s_c, d_c = inp
        return h * d_c[..., None, None] + s_c, h

    _, prev = lax.scan(step, jnp.zeros(states.shape[1:], jnp.float32), (states, chunk_decay))
    y_off = jnp.einsum('bclgn,cbgrpn,bgrcl->bclgrp', cc, prev, jnp.exp(a_cum))
    return (y_diag + y_off).reshape(bsz, seq, n_heads * hd)


def stick_breaking_attention(q, k, v):
    bsz, n_heads, seq, hd = q.shape
    n_blocks = seq // Q_BLOCK
    scale = 1.0 / math.sqrt(hd)
    kf = k.astype(jnp.float32)
    vf = v.astype(jnp.float32)
    key_pos = jnp.arange(seq)

    def block(i):
        start = i * Q_BLOCK
        qb = lax.dynamic_slice_in_dim(q, start, Q_BLOCK, axis=2).astype(jnp.float32)
        z = jnp.einsum('bhqd,bhkd->bhqk', qb, kf) * scale
        qpos = start + jnp.arange(Q_BLOCK)
        earlier = key_pos[None, :] < qpos[:, None]
        log_keep = jnp.where(earlier, jax.nn.log_sigmoid(-z), 0.0)
        later_sum = lax.cumsum(log_keep, axis=3, reverse=True) - log_keep
        weight = jnp.where(earlier, jnp.exp(jax.nn.log_sigmoid(z) + later_sum), 0.0)
        return jnp.einsum('bhqk,bhkd->bhqd', weight, vf)

    out = lax.map(block, jnp.arange(n_blocks))
    return out.transpose(1, 0, 3, 2, 4).reshape(bsz, seq, n_heads * hd)


def hybrid_layer(x, norm_w, w_in, conv_w, conv_b, dt_bias, a_log, d_skip, ssd_norm_w, w_out):
    bsz, seq, _ = x.shape
    h = rmsnorm(x, norm_w)
    proj = h @ w_in
    z, xbc, dt_raw, q, k, v, g = jnp.split(proj, IN_SPLITS, axis=-1)

    xbc = jax.nn.silu(causal_depthwise_conv(xbc, conv_w, conv_b))
    gn = SSD_GROUPS * SSD_STATE
    xs, bm, cm = jnp.split(xbc, (SSD_WIDTH, SSD_WIDTH + gn), axis=-1)
    xs = xs.reshape(bsz, seq, SSD_HEADS, SSD_HEAD_DIM)
    bm = bm.reshape(bsz, seq, SSD_GROUPS, SSD_STATE)
    cm = cm.reshape(bsz, seq, SSD_GROUPS, SSD_STATE)
    dt = jax.nn.softplus(dt_raw.astype(jnp.float32) + dt_bias.astype(jnp.float32))
    a_head = -jnp.exp(a_log.astype(jnp.float32))
    y = ssd_chunked_scan(xs, dt, a_head, bm, cm)
    y = y + (d_skip.astype(jnp.float32)[:, None] * xs.astype(jnp.float32)).reshape(bsz, seq, SSD_WIDTH)
    yg = (y * jax.nn.silu(z.astype(jnp.float32))).reshape(bsz, seq, SSD_GROUPS, SSD_WIDTH // SSD_GROUPS)
    yg = yg * lax.rsqrt(jnp.mean(yg * yg, axis=-1, keepdims=True) + EPS)
    y_ssd = yg.reshape(bsz, seq, SSD_WIDTH) * ssd_norm_w.astype(jnp.float32)

    def heads(t):
        return t.reshape(bsz, seq, SBA_HEADS, SBA_HEAD_DIM).transpose(0, 2, 1, 3)
    o = stick_breaking_attention(heads(q), heads(k), heads(v))
    y_sba = o * jax.nn.silu(g.astype(jnp.float32))

    mix = jnp.concatenate([y_ssd, y_sba], axis=-1).astype(x.dtype)
    return x + mix @ w_out


def setup_inputs(seed: int = 0) -> dict:
    key = jax.random.key(seed)
    ks = jax.random.split(key, 12)
    f32 = jnp.float32
    x = jax.random.normal(ks[0], (BATCH, SEQ, D_MODEL), f32)
    norm_w = 1.0 + 0.02 * jax.random.normal(ks[1], (DEPTH, D_MODEL), f32)
    w_in = jax.random.normal(ks[2], (DEPTH, D_MODEL, IN_COLS), f32) * D_MODEL ** -0.5
    conv_w = jax.random.normal(ks[3], (DEPTH, SSD_CONV, CONV_DIM), f32) * SSD_CONV ** -0.5
    conv_b = 0.02 * jax.random.normal(ks[4], (DEPTH, CONV_DIM), f32)
    dt0 = jnp.exp(jax.random.uniform(ks[5], (DEPTH, SSD_HEADS), f32,
                                     math.log(1e-3), math.log(1e-1)))
    dt_bias = dt0 + jnp.log(-jnp.expm1(-dt0))
    a_log = jnp.log(jax.random.uniform(ks[6], (DEPTH, SSD_HEADS), f32, 1.0, 16.0))
    d_skip = 1.0 + 0.02 * jax.random.normal(ks[7], (DEPTH, SSD_HEADS), f32)
    ssd_norm_w = 1.0 + 0.02 * jax.random.normal(ks[8], (DEPTH, SSD_WIDTH), f32)
    w_out = jax.random.normal(ks[9], (DEPTH, D_MIX, D_MODEL), f32) * D_MIX ** -0.5
    final_norm_w = 1.0 + 0.02 * jax.random.normal(ks[10], (D_MODEL,), f32)
    return {"x": x, "norm_w": norm_w, "w_in": w_in, "conv_w": conv_w, "conv_b": conv_b,
            "dt_bias": dt_bias, "a_log": a_log, "d_skip": d_skip, "ssd_norm_w": ssd_norm_w,
            "w_out": w_out, "final_norm_w": final_norm_w}


def reference(x, norm_w, w_in, conv_w, conv_b, dt_bias, a_log, d_skip, ssd_norm_w, w_out, final_norm_w):
    h = x
    for i in range(DEPTH):
        h = hybrid_layer(h, norm_w[i], w_in[i], conv_w[i], conv_b[i], dt_bias[i], a_log[i],
                         d_skip[i], ssd_norm_w[i], w_out[i])
    return rmsnorm(h, final_norm_w)
```

```python
import math
from contextlib import ExitStack

import numpy as np
import ml_dtypes

import concourse.bass as bass
import concourse.mybir as mybir
from concourse.bass_utils import run_bass_kernel_spmd

F32 = mybir.dt.float32
BF16 = mybir.dt.bfloat16
AF = mybir.ActivationFunctionType
ALU = mybir.AluOpType

NCORES = 8
D_MODEL = 2048
SEQ = 8192
DEPTH = 4
EPS = 1e-6
TOK_PER_CORE = SEQ // NCORES
NEG = -30000.0


class Buf:
    __slots__ = ("name", "last_w", "readers", "excl")

    def __init__(self, name="", excl=False):
        self.name = name
        self.last_w = None
        self.readers = []
        self.excl = excl


class Builder:
    ENGS = ("pe", "act", "dve", "pool", "sp")

    def __init__(self, nc, es):
        self.nc = nc
        self.es = es
        self.eng = {"pe": nc.tensor, "act": nc.scalar, "dve": nc.vector,
                    "pool": nc.gpsimd, "sp": nc.sync}
        self.sems = {}
        self.cnt = {}
        for e in self.ENGS:
            self.sems[e] = es.enter_context(nc.semaphore("s_" + e))
            self.cnt[e] = 0
        self.waited = {e: {} for e in self.ENGS}
        self.nwaits = 0
        self.ninst = 0

    def sb(self, name, shape, dt):
        return self.es.enter_context(self.nc.sbuf_tensor(name, list(shape), dt))

    def ps(self, name, shape, dt):
        return self.es.enter_context(self.nc.psum_tensor(name, list(shape), dt))

    def dma_sem(self, name):
        key = "dma_" + name
        self.sems[key] = self.es.enter_context(self.nc.semaphore(key))
        self.cnt[key] = 0
        return key

    def _wait_for(self, e, deps):
        need = {}
        for h in deps:
            if h is None:
                continue
            k, v = h
            if k == e and e == "pe":
                continue
            if v > need.get(k, 0):
                need[k] = v
        w = self.waited[e]
        for k, v in need.items():
            if w.get(k, 0) < v:
                self.eng[e].wait_ge(self.sems[k], v)
                w[k] = v
                self.nwaits += 1

    @staticmethod
    def _deps(reads, writes):
        deps = []
        for b in reads:
            deps.append(b.last_w)
            if b.excl:
                deps.extend(b.readers)
        for b in writes:
            deps.append(b.last_w)
            deps.extend(b.readers)
        return deps

    @staticmethod
    def _commit(h, reads, writes):
        for b in reads:
            b.readers.append(h)
            if len(b.readers) > 48:
                mx = {}
                for k, v in b.readers:
                    if v > mx.get(k, 0):
                        mx[k] = v
                b.readers = list(mx.items())
        for b in writes:
            b.last_w = h
            b.readers = []

    def op(self, e, fn, reads=(), writes=()):
        self._wait_for(e, self._deps(reads, writes))
        inst = fn(self.eng[e])
        self.cnt[e] += 1
        inst.then_inc(self.sems[e], 1)
        h = (e, self.cnt[e])
        self._commit(h, reads, writes)
        self.ninst += 1
        return h

    def group(self, e, fns, reads=(), writes=()):
        self._wait_for(e, self._deps(reads, writes))
        inst = None
        for fn in fns:
            inst = fn(self.eng[e])
            self.ninst += 1
        self.cnt[e] += 1
        inst.then_inc(self.sems[e], 1)
        h = (e, self.cnt[e])
        self._commit(h, reads, writes)
        return h

    def dma(self, q, semkey, out, in_, reads=(), writes=()):
        deps = []
        for b in reads:
            deps.append(b.last_w)
        for b in writes:
            if b.last_w is not None and b.last_w[0] != semkey:
                deps.append(b.last_w)
            deps.extend(b.readers)
        self._wait_for(q, deps)
        inst = self.eng[q].dma_start(out=out, in_=in_)
        self.cnt[semkey] += 16
        inst.then_inc(self.sems[semkey], 16)
        h = (semkey, self.cnt[semkey])
        self._commit(h, reads, writes)
        self.ninst += 1
        return h

    def final_wait(self, e, bufs):
        self._wait_for(e, [b.last_w for b in bufs])


def _mm(out, lhsT, rhs, start, stop):
    return lambda e: e.matmul(out, lhsT=lhsT, rhs=rhs, start=start, stop=stop)


def emit_rmsnorm(b, xt, Bx, nwt, Bnw, ones32, Bones, sq, Bsq, pbank, Bp,
                 rstd, Brstd, out, Bout, nchunk=16, ncols=512):
    for c in range(nchunk):
        s = c % 2
        b.op("pool" if c % 2 else "dve",
             (lambda c, s: lambda e: e.tensor_tensor(out=sq[:, s, :], in0=xt[:, c, :],
                                                     in1=xt[:, c, :], op=ALU.mult))(c, s),
             reads=[Bx], writes=[Bsq[s]])
        b.op("pe", _mm(pbank[:, 0:ncols], ones32[:], sq[:, s, :], c == 0, c == nchunk - 1),
             reads=[Bsq[s], Bones], writes=[Bp])
    b.op("act", lambda e: e.activation(out=rstd[:], in_=pbank[:, 0:ncols], func=AF.Sqrt,
                                       bias=EPS, scale=1.0 / (nchunk * 128)),
         reads=[Bp], writes=[Brstd])
    b.op("dve", lambda e: e.reciprocal(out=rstd[:], in_=rstd[:]), reads=[Brstd], writes=[Brstd])
    for c in range(nchunk):
        b.op("dve",
             (lambda c: lambda e: e.scalar_tensor_tensor(out=out[:, c, :], in0=xt[:, c, :],
                                                         scalar=nwt[:, c:c + 1], in1=rstd[:],
                                                         op0=ALU.mult, op1=ALU.mult))(c),
             reads=[Bx, Bnw, Brstd], writes=[Bout])


def build_ka():
    nc = bass.Bass("TRN2", target_bir_lowering=False)
    T = TOK_PER_CORE
    xT = nc.dram_tensor("xT", [D_MODEL, T], F32, kind="ExternalInput").ap()
    nw = nc.dram_tensor("nw", [128, 16], F32, kind="ExternalInput").ap()
    hT = nc.dram_tensor("hT", [D_MODEL, T], BF16, kind="ExternalOutput").ap()
    xv = xT.rearrange("(c p) t -> p c t", p=128)
    hv = hT.rearrange("(c p) t -> p c t", p=128)
    with ExitStack() as es:
        b = Builder(nc, es)
        xt = b.sb("xt", [128, 16, 512], F32); Bx = Buf()
        ht = b.sb("ht", [128, 16, 512], BF16); Bh = Buf()
        nwt = b.sb("nwt", [128, 16], F32); Bnw = Buf()
        ones32 = b.sb("ones32", [128, 128], F32); Bones = Buf()
        sq = b.sb("sq", [128, 2, 512], F32); Bsq = [Buf(), Buf()]
        rstd = b.sb("rstd", [128, 512], F32); Brstd = Buf()
        pbank = b.ps("pb", [128, 512], F32); Bp = Buf(excl=True)
        s_x = b.dma_sem("x"); s_nw = b.dma_sem("nw"); s_o = b.dma_sem("o")
        Bo = Buf()
        b.op("pool", lambda e: e.memset(ones32[:], 1.0), writes=[Bones])
        b.dma("sp", s_nw, nwt[:], nw[:, :], writes=[Bnw])
        for half in range(T // 512):
            cs = slice(half * 512, (half + 1) * 512)
            for k0 in range(0, 16, 4):
                b.dma("sp", s_x, xt[:, k0:k0 + 4, :], xv[:, k0:k0 + 4, cs], writes=[Bx])
            emit_rmsnorm(b, xt, Bx, nwt, Bnw, ones32, Bones, sq, Bsq, pbank, Bp,
                         rstd, Brstd, ht, Bh)
            for k0 in range(0, 16, 4):
                b.dma("sp", s_o, hv[:, k0:k0 + 4, cs], ht[:, k0:k0 + 4, :], reads=[Bh], writes=[Bo])
        b.final_wait("sp", [Bo])
    return nc


def build_kc(last):
    nc = bass.Bass("TRN2", target_bir_lowering=False)
    T = TOK_PER_CORE
    mixT = nc.dram_tensor("mixT", [4096, T], BF16, kind="ExternalInput").ap()
    xT = nc.dram_tensor("xT", [D_MODEL, T], F32, kind="ExternalInput").ap()
    wo = nc.dram_tensor("wo", [4096, D_MODEL], F32, kind="ExternalInput").ap()
    snw = nc.dram_tensor("snw", [128, 16], F32, kind="ExternalInput").ap()
    nw = nc.dram_tensor("nw", [128, 16], F32, kind="ExternalInput").ap()
    if last:
        outT = nc.dram_tensor("outT", [D_MODEL, T], F32, kind="ExternalOutput").ap()
        ov = outT.rearrange("(c p) t -> p c t", p=128)
    else:
        xo = nc.dram_tensor("xo", [D_MODEL, T], F32, kind="ExternalOutput").ap()
        hT = nc.dram_tensor("hT", [D_MODEL, T], BF16, kind="ExternalOutput").ap()
        xov = xo.rearrange("(c p) t -> p c t", p=128)
        hv = hT.rearrange("(c p) t -> p c t", p=128)
    mv = mixT.rearrange("(c p) t -> p c t", p=128)
    xv = xT.rearrange("(c p) t -> p c t", p=128)
    wv = wo.rearrange("(k p) c -> p k c", p=128)
    with ExitStack() as es:
        b = Builder(nc, es)
        mt = b.sb("mt", [128, 32, 512], BF16); Bm = Buf()
        xt = b.sb("xt", [128, 16, 512], F32); Bx = Buf()
        ot = b.sb("ot", [128, 16, 512], F32 if last else BF16); Bo = Buf()
        wt = b.sb("wt", [128, 2, 32, 256], BF16); Bw = [Buf(), Buf()]
        snwt = b.sb("snwt", [128, 16], F32); Bsnw = Buf()
        nwt = b.sb("nwt", [128, 16], F32); Bnw = Buf()
        ones32 = b.sb("ones32", [128, 128], F32); Bones = Buf()
        sq = b.sb("sq", [128, 2, 512], F32); Bsq = [Buf(), Buf()]
        rstd = b.sb("rstd", [128, 512], F32); Brstd = Buf()
        pn = b.ps("pn", [128, 512], F32); Bpn = Buf(excl=True)
        pm = [b.ps("pm%d" % i, [128, 512], F32) for i in range(2)]; Bpm = [Buf(excl=True), Buf(excl=True)]
        s_m = b.dma_sem("m"); s_x = b.dma_sem("x"); s_c = b.dma_sem("c")
        s_w = [b.dma_sem("w0"), b.dma_sem("w1")]
        s_o1 = b.dma_sem("o1"); s_o2 = b.dma_sem("o2")
        Bd1 = Buf(); Bd2 = Buf()
        b.op("pool", lambda e: e.memset(ones32[:], 1.0), writes=[Bones])
        b.dma("sp", s_c, snwt[:], snw[:, :], writes=[Bsnw])
        b.dma("sp", b.dma_sem("c2"), nwt[:], nw[:, :], writes=[Bnw])
        wi = 0
        for half in range(T // 512):
            cs = slice(half * 512, (half + 1) * 512)
            for k0 in range(0, 32, 4):
                b.dma("sp", s_m, mt[:, k0:k0 + 4, :], mv[:, k0:k0 + 4, cs], writes=[Bm])
            for k0 in range(0, 16, 4):
                b.dma("sp", s_x, xt[:, k0:k0 + 4, :], xv[:, k0:k0 + 4, cs], writes=[Bx])
            def load_w(cb2, slot):
                for k0 in range(0, 32, 8):
                    b.dma("pool", s_w[slot], wt[:, slot, k0:k0 + 8, :],
                          wv[:, k0:k0 + 8, cb2 * 256:(cb2 + 1) * 256], writes=[Bw[slot]])
            load_w(0, wi % 2)
            for g in range(4):
                for c in range(4):
                    s = c % 2
                    ch = 4 * g + c
                    b.op("dve" if c % 2 else "pool",
                         (lambda ch, s: lambda e: e.tensor_tensor(out=sq[:, s, :], in0=mt[:, ch, :],
                                                                  in1=mt[:, ch, :], op=ALU.mult))(ch, s),
                         reads=[Bm], writes=[Bsq[s]])
                    b.op("pe", _mm(pn[:], ones32[:], sq[:, s, :], c == 0, c == 3),
                         reads=[Bsq[s], Bones], writes=[Bpn])
                b.op("act", lambda e: e.activation(out=rstd[:], in_=pn[:], func=AF.Sqrt,
                                                   bias=EPS, scale=1.0 / 512.0),
                     reads=[Bpn], writes=[Brstd])
                b.op("dve", lambda e: e.reciprocal(out=rstd[:], in_=rstd[:]),
                     reads=[Brstd], writes=[Brstd])
                for c in range(4):
                    ch = 4 * g + c
                    b.op("dve",
                         (lambda ch: lambda e: e.scalar_tensor_tensor(
                             out=mt[:, ch, :], in0=mt[:, ch, :], scalar=snwt[:, ch:ch + 1],
                             in1=rstd[:], op0=ALU.mult, op1=ALU.mult))(ch),
                         reads=[Bm, Bsnw, Brstd], writes=[Bm])
            for cb2 in range(8):
                slot = wi % 2
                if cb2 + 1 < 8:
                    load_w(cb2 + 1, (wi + 1) % 2)
                for sub in range(2):
                    cblk = 2 * cb2 + sub
                    pb = pm[cblk % 2]; Bpb = Bpm[cblk % 2]
                    b.group("pe", [_mm(pb[:], wt[:, slot, k, sub * 128:(sub + 1) * 128], mt[:, k, :],
                                       k == 0, k == 31) for k in range(32)],
                            reads=[Bw[slot], Bm], writes=[Bpb])
                    b.op("dve", (lambda cblk, pb: lambda e: e.tensor_tensor(
                        out=xt[:, cblk, :], in0=xt[:, cblk, :], in1=pb[:], op=ALU.add))(cblk, pb),
                        reads=[Bx, Bpb], writes=[Bx])
                wi += 1
            if not last:
                for k0 in range(0, 16, 4):
                    b.dma("sp", s_o1, xov[:, k0:k0 + 4, cs], xt[:, k0:k0 + 4, :], reads=[Bx], writes=[Bd1])
            emit_rmsnorm(b, xt, Bx, nwt, Bnw, ones32, Bones, sq, Bsq, pn, Bpn,
                         rstd, Brstd, ot, Bo)
            for k0 in range(0, 16, 4):
                b.dma("sp", s_o2, (ov if last else hv)[:, k0:k0 + 4, cs], ot[:, k0:k0 + 4, :],
                      reads=[Bo], writes=[Bd2])
        b.final_wait("sp", [Bd1, Bd2])
    return nc


NFM = 12
WCOLS = NFM * 128 + 260


def build_kb(NT, phase=3):
    NTOK = NT * 512
    NKB = NTOK // 128
    nc = bass.Bass("TRN2", target_bir_lowering=False)
    hT = nc.dram_tensor("hT", [D_MODEL, NTOK], BF16, kind="ExternalInput").ap()
    w = nc.dram_tensor("w", [D_MODEL, WCOLS], F32, kind="ExternalInput").ap()
    cw = nc.dram_tensor("cw", [128, 4, 4], F32, kind="ExternalInput").ap()
    cbias = nc.dram_tensor("cb", [128, 4], F32, kind="ExternalInput").ap()
    dtb = nc.dram_tensor("dtb", [128, 4], F32, kind="ExternalInput").ap()
    alog = nc.dram_tensor("alog", [128, 4], F32, kind="ExternalInput").ap()
    dsk = nc.dram_tensor("dsk", [128, 2], F32, kind="ExternalInput").ap()
    mix = nc.dram_tensor("mix", [512, NTOK], BF16, kind="ExternalOutput").ap()
    hv = hT.rearrange("(k p) t -> p k t", p=128)
    wv = w.rearrange("(k p) c -> p k c", p=128)
    mixv = mix.rearrange("(b p) t -> p b t", p=128)
    scale = 1.0 / math.sqrt(128.0)

    with ExitStack() as es:
        b = Builder(nc, es)
        Wb = b.sb("Wb", [128, 16, WCOLS], BF16); BW = Buf()
        KT = b.sb("KT", [128, 2, NTOK], BF16); BKT = [Buf() for _ in range(NT)]
        Vt = b.sb("Vt", [128, NKB, 256], BF16); BV = [Buf() for _ in range(NT)]
        hTt = b.sb("hTt", [128, 16, 512], BF16); BhT = Buf()
        u = b.sb("u", [128, 4, 515], F32); Bu = Buf()
        xsF = b.sb("xsF", [128, 4, 512], F32); BxsF = Buf()
        bcB = b.sb("bcB", [128, 2, 512], BF16); BbcB = Buf()
        zs = b.sb("zs", [128, 2, 512], F32); Bzs = Buf()
        gs = b.sb("gs", [128, 2, 512], F32); Bgs = Buf()
        QT = b.sb("QT", [128, 2, 512], BF16); BQT = Buf()
        dtr = b.sb("dtr", [128, 4, 4], F32); Bdtr = Buf()
        dtv = b.sb("dtv", [128, 4, 4], F32); Bdtv = Buf()
        dav = b.sb("dav", [128, 4, 4], F32); Bdav = Buf()
        cwt = b.sb("cwt", [128, 4, 4], F32); Bcw = Buf()
        cbt = b.sb("cbt", [128, 4], F32); Bcb = Buf()
        dtbt = b.sb("dtbt", [128, 4], F32); Bdtb = Buf()
        At = b.sb("At", [128, 4], F32); BA = Buf()
        dskt = b.sb("dskt", [128, 2], F32); Bdsk = Buf()
        ident32 = b.sb("ident32", [128, 128], F32)
        identb = b.sb("identb", [128, 128], BF16)
        triS = b.sb("triS", [128, 128], F32)
        triI = b.sb("triI", [128, 128], F32)
        nUi = b.sb("nUi", [128, 128], BF16)
        nOnes = b.sb("nOnes", [128, 128], BF16)
        negm = b.sb("negm", [128, 4, 512], BF16)
        zer = b.sb("zer", [128, 512], BF16)
        Bconst = Buf()
        xdt = b.sb("xdt", [128, 4, 64], BF16); Bxdt = Buf()
        xdte = b.sb("xdte", [128, 4, 64], BF16); Bxdte = Buf()
        Btok = b.sb("Btok", [128, 128], BF16); BBtok = Buf()
        trida = b.sb("trida", [128, 4, 128], F32); Btrida = Buf()
        dabc = b.sb("dabc", [128, 4, 128], F32); Bdabc = Buf()
        dec = b.sb("dec", [128, 4, 128], F32); Bdec = Buf()
        ebc = b.sb("ebc", [128, 4, 128], F32); Bebc = Buf()
        CBm = b.sb("CBm", [128, 128], F32); BCBm = Buf()
        Mt = b.sb("Mt", [128, 4, 128], BF16); BMt = Buf()
        Cs = b.sb("Cs", [128, 4, 128], BF16); BCs = Buf()
        Hs = b.sb("Hs", [128, 4, 64], F32); BHs = Buf()
        Hb = b.sb("Hb", [128, 4, 64], BF16); BHb = Buf()
        ytmp = b.sb("ytmp", [128, 2, 128], F32); Bytmp = Buf()
        ygT = b.sb("ygT", [128, 2, 512], BF16); BygT = Buf()
        Eb = b.sb("Eb", [128, 512], F32); BE = Buf()
        Lb = b.sb("Lb", [128, 2, 512], BF16); BL = [Buf(), Buf()]
        Sb = b.sb("Sb", [128, 2, 512], BF16); BS = [Buf(), Buf()]
        Wt = b.sb("Wt", [128, 2, 512], BF16); BWt = [Buf(), Buf()]
        aoT = b.sb("aoT", [128, 2, 512], BF16); BaoT = Buf()
        pfm = [b.ps("pfm%d" % i, [128, 512], F32) for i in range(2)]; Bpfm = [Buf(excl=True), Buf(excl=True)]
        p2 = b.ps("p2", [128, 512], F32); B2a = Buf(excl=True); B2b = B2a
        p3 = b.ps("p3", [128, 512], F32); B3a = Buf(excl=True); B3b = B3a
        pA = [b.ps("pA%d" % i, [128, 512], F32) for i in range(2)]; BpA = [Buf(excl=True), Buf(excl=True)]
        pB = b.ps("pB", [128, 512], F32); BpB = Buf(excl=True)
        pO = b.ps("pO", [128, 512], F32); BpO = Buf(excl=True)
        s_w = b.dma_sem("w"); s_c = b.dma_sem("c")
        s_h = b.dma_sem("h")
        s_oy = b.dma_sem("oy"); s_oa = b.dma_sem("oa")
        Boy = Buf(); Boa = Buf()

        P = "pool"
        b.op(P, lambda e: e.memset(ident32[:], 0.0), writes=[Bconst])
        b.op(P, lambda e: e.affine_select(out=ident32[:], in_=ident32[:], pattern=[[-1, 128]],
                                          compare_op=ALU.not_equal, fill=1.0, base=0,
                                          channel_multiplier=1), reads=[Bconst], writes=[Bconst])
        b.op(P, lambda e: e.tensor_copy(out=identb[:], in_=ident32[:]), reads=[Bconst], writes=[Bconst])
        b.op(P, lambda e: e.memset(triS[:], 1.0), writes=[Bconst])
        b.op(P, lambda e: e.affine_select(out=triS[:], in_=triS[:], pattern=[[-1, 128]],
                                          compare_op=ALU.is_ge, fill=0.0, base=-1,
                                          channel_multiplier=1), reads=[Bconst], writes=[Bconst])
        b.op(P, lambda e: e.memset(triI[:], 1.0), writes=[Bconst])
        b.op(P, lambda e: e.affine_select(out=triI[:], in_=triI[:], pattern=[[1, 128]],
                                          compare_op=ALU.is_ge, fill=0.0, base=0,
                                          channel_multiplier=-1), reads=[Bconst], writes=[Bconst])
        b.op(P, lambda e: e.memset(nUi[:], -1.0), writes=[Bconst])
        b.op(P, lambda e: e.affine_select(out=nUi[:], in_=nUi[:], pattern=[[-1, 128]],
                                          compare_op=ALU.is_ge, fill=0.0, base=0,
                                          channel_multiplier=1), reads=[Bconst], writes=[Bconst])
        b.op(P, lambda e: e.memset(nOnes[:], -1.0), writes=[Bconst])
        b.op(P, lambda e: e.memset(zer[:], 0.0), writes=[Bconst])
        b.op(P, lambda e: e.memset(negm[:], 0.0), writes=[Bconst])
        for r in range(4):
            b.op(P, (lambda r: lambda e: e.affine_select(
                out=negm[:, r, :], in_=negm[:, r, :], pattern=[[1, 512]], compare_op=ALU.is_ge,
                fill=NEG, base=-128 * r - 1, channel_multiplier=-1))(r),
                reads=[Bconst], writes=[Bconst])
        b.op(P, lambda e: e.memset(Hs[:], 0.0), writes=[BHs])
        b.op(P, lambda e: e.memset(Hb[:], 0.0), writes=[BHb])
        b.op(P, lambda e: e.memset(u[:, :, 0:3], 0.0), writes=[Bu])
        b.dma("sp", b.dma_sem("c0"), cwt[:], cw[:, :, :], writes=[Bcw])
        b.dma("sp", b.dma_sem("c1"), cbt[:], cbias[:, :], writes=[Bcb])
        b.dma("sp", b.dma_sem("c2"), dtbt[:], dtb[:, :], writes=[Bdtb])
        b.dma("sp", b.dma_sem("c3"), At[:], alog[:, :], writes=[BA])
        b.dma("sp", b.dma_sem("c4"), dskt[:], dsk[:, :], writes=[Bdsk])
        b.op("act", lambda e: e.activation(out=At[:], in_=At[:], func=AF.Exp), reads=[BA], writes=[BA])
        b.op("dve", lambda e: e.tensor_scalar(out=At[:], in0=At[:], scalar1=-1.0, scalar2=None,
                                              op0=ALU.mult), reads=[BA], writes=[BA])
        for k0 in range(0, 16, 2):
            b.dma("pool", s_w, Wb[:, k0:k0 + 2, :], wv[:, k0:k0 + 2, :], writes=[BW])

        def load_h(i):
            for k0 in range(0, 16, 4):
                b.dma("sp", s_h, hTt[:, k0:k0 + 4, :], hv[:, k0:k0 + 4, i * 512:(i + 1) * 512],
                      writes=[BhT])

        load_h(0)
        fmi = 0
        for i in range(NT):
            tcs = slice(i * 512, (i + 1) * 512)
            for blk in range(NFM):
                pb = pfm[fmi % 2]; Bpb = Bpfm[fmi % 2]; fmi += 1
                b.group("pe", [_mm(pb[:], Wb[:, k, blk * 128:(blk + 1) * 128], hTt[:, k, :],
                                   k == 0, k == 15) for k in range(16)],
                        reads=[BW, BhT], writes=[Bpb])
                if blk < 2:
                    b.op("act", (lambda blk, pb: lambda e: e.activation(
                        out=zs[:, blk, :], in_=pb[:], func=AF.Silu))(blk, pb),
                        reads=[Bpb], writes=[Bzs])
                elif blk < 6:
                    b.op("dve", (lambda blk, pb: lambda e: e.tensor_copy(
                        out=u[:, blk - 2, 3:515], in_=pb[:]))(blk, pb),
                        reads=[Bpb], writes=[Bu])
                elif blk < 8:
                    b.op("dve", (lambda blk, pb: lambda e: e.tensor_scalar(
                        out=QT[:, blk - 6, :], in0=pb[:], scalar1=scale, scalar2=None,
                        op0=ALU.mult))(blk, pb), reads=[Bpb], writes=[BQT])
                elif blk < 10:
                    b.op("dve", (lambda blk, pb: lambda e: e.tensor_copy(
                        out=KT[:, blk - 8, tcs], in_=pb[:]))(blk, pb),
                        reads=[Bpb], writes=[BKT[i]])
                else:
                    b.op("act", (lambda blk, pb: lambda e: e.activation(
                        out=gs[:, blk - 10, :], in_=pb[:], func=AF.Silu))(blk, pb),
                        reads=[Bpb], writes=[Bgs])
            for tb in range(4):
                kb = 4 * i + tb
                ts_ = slice(tb * 128, (tb + 1) * 128)
                b.group("pe", [_mm(p2[:, 0:256], hTt[:, k, ts_], Wb[:, k, 1536:1792],
                                   k == 0, k == 15) for k in range(16)],
                        reads=[BW, BhT], writes=[B2a])
                b.op("dve", (lambda kb: lambda e: e.tensor_copy(out=Vt[:, kb, :], in_=p2[:, 0:256]))(kb),
                     reads=[B2a], writes=[BV[i]])
                b.group("pe", [_mm(p3[:, 0:4], hTt[:, k, ts_], Wb[:, k, 1792:1796],
                                   k == 0, k == 15) for k in range(16)],
                        reads=[BW, BhT], writes=[B3a])
                b.op("dve", (lambda tb: lambda e: e.tensor_tensor(
                    out=dtr[:, tb, :], in0=p3[:, 0:4], in1=dtbt[:], op=ALU.add))(tb),
                    reads=[B3a, Bdtb], writes=[Bdtr])
            if i + 1 < NT:
                load_h(i + 1)
            b.op("act", lambda e: e.activation(out=dtv[:], in_=dtr[:], func=AF.Exp),
                 reads=[Bdtr], writes=[Bdtv])
            b.op("act", lambda e: e.activation(out=dtv[:], in_=dtv[:], func=AF.Ln, bias=1.0, scale=1.0),
                 reads=[Bdtv], writes=[Bdtv])
            b.op("dve", lambda e: e.tensor_tensor(
                out=dav[:], in0=dtv[:], in1=At[:].unsqueeze(1).to_broadcast([128, 4, 4]),
                op=ALU.mult), reads=[Bdtv, BA], writes=[Bdav])
            for cblk in range(4):
                eng = "dve"
                b.op(eng, (lambda cblk: lambda e: e.tensor_scalar(
                    out=xsF[:, cblk, :], in0=u[:, cblk, 0:512], scalar1=cwt[:, cblk, 0:1],
                    scalar2=cbt[:, cblk:cblk + 1], op0=ALU.mult, op1=ALU.add))(cblk),
                    reads=[Bu, Bcw, Bcb], writes=[BxsF])
                for j in range(1, 4):
                    b.op(eng, (lambda cblk, j: lambda e: e.scalar_tensor_tensor(
                        out=xsF[:, cblk, :], in0=u[:, cblk, j:j + 512], scalar=cwt[:, cblk, j:j + 1],
                        in1=xsF[:, cblk, :], op0=ALU.mult, op1=ALU.add))(cblk, j),
                        reads=[Bu, Bcw, BxsF], writes=[BxsF])
            b.op("act", lambda e: e.activation(out=xsF[:], in_=xsF[:], func=AF.Silu),
                 reads=[BxsF], writes=[BxsF])
            b.op("pool", lambda e: e.tensor_copy(out=u[:, :, 0:3], in_=u[:, :, 512:515]),
                 reads=[Bu], writes=[Bu])
            b.op("pool", lambda e: e.tensor_copy(out=bcB[:], in_=xsF[:, 2:4, :]),
                 reads=[BxsF], writes=[BbcB])
            for tb in range(4 if phase > 1 else 0):
                ts_ = slice(tb * 128, (tb + 1) * 128)
                b.group("pe", [(lambda blk: lambda e: e.transpose(
                    out=p3[:, blk * 128:(blk + 1) * 128], in_=xsF[:, blk, ts_], identity=ident32[:]))(blk)
                    for blk in range(3)], reads=[BxsF, Bconst], writes=[B3a])
                b.op("dve", (lambda tb: lambda e: e.tensor_tensor(
                    out=xdt[:], in0=p3[:, 0:256].rearrange("p (h d) -> p h d", h=4),
                    in1=dtv[:, tb, :].unsqueeze(2).to_broadcast([128, 4, 64]), op=ALU.mult))(tb),
                    reads=[B3a, Bdtv], writes=[Bxdt])
                b.op("dve", lambda e: e.tensor_copy(out=Btok[:], in_=p3[:, 256:384]),
                     reads=[B3a], writes=[BBtok])
                if phase < 1.1500000000000001:
                    continue
                b.op("pe", _mm(p3[:, 384:512], bcB[:, 0, ts_], bcB[:, 1, ts_], True, True),
                     reads=[BbcB], writes=[B3b])
                b.op("dve", lambda e: e.tensor_tensor(out=CBm[:], in0=p3[:, 384:512], in1=triI[:],
                                                      op=ALU.mult),
                     reads=[B3b, Bconst], writes=[BCBm])
                if phase < 1.25:
                    continue
                b.op("dve", (lambda tb: lambda e: e.tensor_tensor(
                    out=trida[:], in0=triS[:].unsqueeze(1).to_broadcast([128, 4, 128]),
                    in1=dav[:, tb, :].unsqueeze(2).to_broadcast([128, 4, 128]), op=ALU.mult))(tb),
                    reads=[Bconst, Bdav], writes=[Btrida])
                b.op("dve", (lambda tb: lambda e: e.tensor_copy(
                    out=dabc[:], in_=dav[:, tb, :].unsqueeze(2).to_broadcast([128, 4, 128])))(tb),
                    reads=[Bdav], writes=[Bdabc])
                if phase < 1.35:
                    continue
                pseg = pfm[fmi % 2]; Bpseg = Bpfm[fmi % 2]; fmi += 1
                pac = pfm[fmi % 2]; Bpac = Bpfm[fmi % 2]; fmi += 1
                b.group("pe", [_mm(pseg[:, h * 128:(h + 1) * 128], trida[:, h, :], triI[:], True, True)
                               for h in range(4)], reads=[Btrida, Bconst], writes=[Bpseg])
                b.group("pe", [_mm(pac[:, h * 128:(h + 1) * 128], dabc[:, h, :], triI[:], True, True)
                               for h in range(4)], reads=[Bdabc, Bconst], writes=[Bpac])
                if phase < 1.45:
                    continue
                b.op("act", (lambda pseg: lambda e: e.activation(
                    out=dec[:].rearrange("p h l -> p (h l)"), in_=pseg[:], func=AF.Exp))(pseg),
                    reads=[Bpseg], writes=[Bdec])
                b.op("act", (lambda pac: lambda e: e.activation(
                    out=ebc[:].rearrange("p h l -> p (h l)"), in_=pac[:], func=AF.Exp))(pac),
                    reads=[Bpac], writes=[Bebc])
                if phase < 1.55:
                    continue
                b.op("dve", lambda e: e.tensor_tensor(
                    out=Mt[:], in0=dec[:], in1=CBm[:].unsqueeze(1).to_broadcast([128, 4, 128]),
                    op=ALU.mult), reads=[Bdec, BCBm], writes=[BMt])
                b.op("dve", (lambda ts_: lambda e: e.tensor_tensor(
                    out=Cs[:], in0=ebc[:], in1=xsF[:, 3, ts_].unsqueeze(1).to_broadcast([128, 4, 128]),
                    op=ALU.mult))(ts_), reads=[Bebc, BxsF], writes=[BCs])
                if phase < 1.6500000000000001:
                    continue
                fns = []
                for blk in range(2):
                    for hh in range(2):
                        h = 2 * blk + hh
                        o = p2[hh * 64:(hh + 1) * 64, 256 + blk * 128:256 + (blk + 1) * 128]
                        fns.append(_mm(o, xdt[:, h, :], Mt[:, h, :], True, False))
                        fns.append(_mm(o, Hb[:, h, :], Cs[:, h, :], False, True))
                b.group("pe", fns, reads=[Bxdt, BMt, BHb, BCs], writes=[B2b])
                if phase < 1.75:
                    continue
                for blk in range(2):
                    b.op("dve", (lambda blk, ts_: lambda e: e.scalar_tensor_tensor(
                        out=ytmp[:, blk, :], in0=xsF[:, blk, ts_], scalar=dskt[:, blk:blk + 1],
                        in1=p2[:, 256 + blk * 128:256 + (blk + 1) * 128],
                        op0=ALU.mult, op1=ALU.add))(blk, ts_),
                        reads=[BxsF, Bdsk, B2b], writes=[Bytmp])
                b.op("pool", (lambda ts_: lambda e: e.tensor_tensor(
                    out=ygT[:, :, ts_], in0=ytmp[:], in1=zs[:, :, ts_], op=ALU.mult))(ts_),
                    reads=[Bytmp, Bzs], writes=[BygT])
                if phase < 1.85:
                    continue
                b.op("dve", lambda e: e.tensor_tensor(
                    out=xdte[:], in0=xdt[:], in1=dec[:, :, 127:128].to_broadcast([128, 4, 64]),
                    op=ALU.mult), reads=[Bxdt, Bdec], writes=[Bxdte])
                b.op("pe", _mm(p3[:, 0:256], Btok[:], xdte[:].rearrange("p h d -> p (h d)"), True, True),
                     reads=[BBtok, Bxdte], writes=[B3a])
                if phase < 1.95:
                    continue
                b.op("dve", lambda e: e.tensor_tensor(
                    out=Hs[:], in0=Hs[:], in1=ebc[:, :, 127:128].to_broadcast([128, 4, 64]),
                    op=ALU.mult), reads=[BHs, Bebc], writes=[BHs])
                b.op("dve", lambda e: e.tensor_tensor(
                    out=Hs[:], in0=Hs[:], in1=p3[:, 0:256].rearrange("p (h d) -> p h d", h=4),
                    op=ALU.add), reads=[BHs, B3a], writes=[BHs])
                b.op("pool", lambda e: e.tensor_copy(out=Hb[:], in_=Hs[:]), reads=[BHs], writes=[BHb])
            b.dma("sp", s_oy, mixv[:, 0:2, tcs], ygT[:], reads=[BygT], writes=[Boy])

            nk = 4 * i + 4
            steps = [(hd, k) for hd in range(2) for k in range(nk)]
            NS = len(steps)
            if phase < 3:
                continue

            def kslice(k):
                kb = nk - 1 - k
                return kb, slice(kb * 128, (kb + 1) * 128)

            def emit_A(j):
                hd, k = steps[j]
                kb, ks = kslice(k)
                fns = [_mm(pA[j % 2][:], KT[:, hd, ks], QT[:, hd, :], True, k >= 4)]
                if k < 4:
                    fns.append(_mm(pA[j % 2][:], identb[:], negm[:, 3 - k, :], False, True))
                b.group("pe", fns, reads=[BKT[kb // 4], BQT, Bconst], writes=[BpA[j % 2]])

            def emit_expA_ln(j):
                b.op("act", lambda e: e.activation(out=Eb[:], in_=pA[j % 2][:], func=AF.Exp),
                     reads=[BpA[j % 2]], writes=[BE])
                b.op("act", lambda e: e.activation(out=Lb[:, j % 2, :], in_=Eb[:], func=AF.Ln,
                                                   bias=1.0, scale=1.0),
                     reads=[BE], writes=[BL[j % 2]])

            def emit_B(j):
                hd, k = steps[j]
                kb, ks = kslice(k)
                pairs = [(KT[:, hd, ks], QT[:, hd, :]), (nUi[:], Lb[:, j % 2, :])]
                reads = [BKT[kb // 4], BQT, Bconst, BL[j % 2]]
                if k > 0:
                    pairs.append((nOnes[:], Sb[:, j % 2, :]))
                    reads.append(BS[j % 2])
                if k < 4:
                    pairs.append((identb[:], negm[:, 3 - k, :]))
                n = len(pairs)
                fns = [_mm(pB[:], l_, r_, ii == 0, ii == n - 1) for ii, (l_, r_) in enumerate(pairs)]
                b.group("pe", fns, reads=reads, writes=[BpB])

            def emit_S(j):
                hd, k = steps[j]
                if k == nk - 1:
                    return
                src = zer[:] if k == 0 else Sb[:, j % 2, :]
                rd = [Bconst] if k == 0 else [BS[j % 2]]
                b.op("pool", lambda e: e.tensor_tensor(out=Sb[:, (j + 1) % 2, :], in0=src,
                                                       in1=Lb[:, j % 2, :], op=ALU.add),
                     reads=rd + [BL[j % 2]], writes=[BS[(j + 1) % 2]])

            def emit_expB(j):
                b.op("act", lambda e: e.activation(out=Wt[:, j % 2, :], in_=pB[:], func=AF.Exp),
                     reads=[BpB], writes=[BWt[j % 2]])

            def emit_PV(j):
                hd, k = steps[j]
                kb, ks = kslice(k)
                b.op("pe", _mm(pO[:], Vt[:, kb, hd * 128:(hd + 1) * 128], Wt[:, j % 2, :],
                               k == 0, k == nk - 1),
                     reads=[BV[kb // 4], BWt[j % 2]], writes=[BpO])
                if k == nk - 1:
                    b.op("dve", (lambda hd: lambda e: e.tensor_tensor(
                        out=aoT[:, hd, :], in0=pO[:], in1=gs[:, hd, :], op=ALU.mult))(hd),
                        reads=[BpO, Bgs], writes=[BaoT])

            emit_A(0)
            if NS > 1:
                emit_A(1)
            emit_expA_ln(0)
            for j in range(NS):
                if j + 2 < NS:
                    emit_A(j + 2)
                if j + 1 < NS:
                    emit_expA_ln(j + 1)
                emit_B(j)
                emit_S(j)
                emit_expB(j)
                if j >= 1:
                    emit_PV(j - 1)
            emit_PV(NS - 1)
            b.dma("sp", s_oa, mixv[:, 2:4, tcs], aoT[:], reads=[BaoT], writes=[Boa])
        b.final_wait("sp", [Boy, Boa])
        build_kb.stats = (b.ninst, b.nwaits)
    return nc


_CACHE = {}


def _get(name, fn):
    if name not in _CACHE:
        _CACHE[name] = fn()
    return _CACHE[name]


def _run(nc, in_maps):
    res = run_bass_kernel_spmd(nc, in_maps, core_ids=list(range(NCORES)))
    return res.results


def _pc(v):
    return np.ascontiguousarray(np.asarray(v, np.float32).reshape(16, 128).T)


def _kb_params(c, l, w_in, conv_w, conv_b, dt_bias, a_log, d_skip):
    g = c // 2
    W = w_in[l]
    z0 = 0; x0 = 2048; B0 = 4096; C0 = 4608; dt0 = 5120
    q0 = 5152; k0 = q0 + 2048; v0 = k0 + 2048; g0 = v0 + 2048
    cols = np.concatenate([
        np.arange(z0 + 256 * c, z0 + 256 * (c + 1)),
        np.arange(x0 + 256 * c, x0 + 256 * (c + 1)),
        np.arange(B0 + 128 * g, B0 + 128 * (g + 1)),
        np.arange(C0 + 128 * g, C0 + 128 * (g + 1)),
        np.arange(q0 + 256 * c, q0 + 256 * (c + 1)),
        np.arange(k0 + 256 * c, k0 + 256 * (c + 1)),
        np.arange(g0 + 256 * c, g0 + 256 * (c + 1)),
        np.arange(v0 + 256 * c, v0 + 256 * (c + 1)),
        np.arange(dt0 + 4 * c, dt0 + 4 * (c + 1)),
    ])
    wc = np.ascontiguousarray(W[:, cols])
    cch = np.concatenate([
        np.arange(256 * c, 256 * (c + 1)),
        np.arange(2048 + 128 * g, 2048 + 128 * (g + 1)),
        np.arange(2560 + 128 * g, 2560 + 128 * (g + 1)),
    ])
    cwc = conv_w[l][:, cch]
    cw = np.ascontiguousarray(cwc.reshape(4, 4, 128).transpose(2, 1, 0))
    cb = np.ascontiguousarray(conv_b[l][cch].reshape(4, 128).T)
    heads = np.arange(4 * c, 4 * c + 4)
    dtb = np.ascontiguousarray(np.broadcast_to(dt_bias[l][heads][None, :], (128, 4)))
    al = np.ascontiguousarray(np.broadcast_to(a_log[l][heads][None, :], (128, 4)))
    dsk = np.ascontiguousarray(np.repeat(d_skip[l][heads], 64).reshape(2, 128).T)
    return {"w": wc, "cw": cw, "cb": cb, "dtb": dtb, "alog": al, "dsk": dsk}


def kernel(x, norm_w, w_in, conv_w, conv_b, dt_bias, a_log, d_skip, ssd_norm_w, w_out,
           final_norm_w):
    f = lambda a: np.asarray(a, dtype=np.float32)
    x = f(x); norm_w = f(norm_w); w_in = f(w_in); conv_w = f(conv_w); conv_b = f(conv_b)
    dt_bias = f(dt_bias); a_log = f(a_log); d_skip = f(d_skip); ssd_norm_w = f(ssd_norm_w)
    w_out = f(w_out); final_norm_w = f(final_norm_w)
    T = TOK_PER_CORE
    xT = np.ascontiguousarray(x[0].T)
    xs = [np.ascontiguousarray(xT[:, c * T:(c + 1) * T]) for c in range(NCORES)]
    ka = _get("ka", build_ka)
    kb = _get("kb", lambda: build_kb(SEQ // 512))
    kc = _get("kc", lambda: build_kc(False))
    kcl = _get("kcl", lambda: build_kc(True))
    r = _run(ka, [{"xT": xs[c], "nw": _pc(norm_w[0])} for c in range(NCORES)])
    hT = np.concatenate([r[c]["hT"] for c in range(NCORES)], axis=1)
    out = None
    for l in range(DEPTH):
        r = _run(kb, [dict(hT=hT, **_kb_params(c, l, w_in, conv_w, conv_b, dt_bias, a_log, d_skip))
                      for c in range(NCORES)])
        mixT = np.concatenate([r[c]["mix"][0:256] for c in range(NCORES)] +
                              [r[c]["mix"][256:512] for c in range(NCORES)], axis=0)
        last = l == DEPTH - 1
        nw_next = final_norm_w if last else norm_w[l + 1]
        ins = [{"mixT": np.ascontiguousarray(mixT[:, c * T:(c + 1) * T]), "xT": xs[c],
                "wo": w_out[l], "snw": _pc(ssd_norm_w[l]), "nw": _pc(nw_next)}
               for c in range(NCORES)]
        r = _run(kcl if last else kc, ins)
        if last:
            out = np.concatenate([r[c]["outT"] for c in range(NCORES)], axis=1)
        else:
            xs = [r[c]["xo"] for c in range(NCORES)]
            hT = np.concatenate([r[c]["hT"] for c in range(NCORES)], axis=1)
    return np.ascontiguousarray(out.T)[None].astype(np.float32)
```
